# Optimizing a Trainium2 kernel written in Bass

```python
import math
import jax, jax.numpy as jnp
from jax import lax
import numpy as np

D_MODEL = 2048
BATCH = 1
SEQ = 8192
DEPTH = 1

N_MEM = 256
D_FF = 5632
MIX_WIDTH = D_MODEL
DIFF_WIDTH = MIX_WIDTH // 2
CONV_WIDTH = MIX_WIDTH - DIFF_WIDTH
DIFF_HEADS = 8
DIFF_VDIM = DIFF_WIDTH // DIFF_HEADS
DIFF_QKDIM = DIFF_VDIM // 2
CONV_GROUPS = 8
CONV_K = 31
ROT_DIM = DIFF_QKDIM // 4
ROPE_THETA = 500000.0
XATTN_HEADS = 4
XATTN_HDIM = 128
XATTN_WIDTH = XATTN_HEADS * XATTN_HDIM
Q_BLOCK = 128
EPS = 1e-6
IN_COLS = 3 * DIFF_WIDTH + 2 * CONV_WIDTH

kernel_name = "hybrid_diffattn_conformer_conv_macaron"


def rms_norm(x, g):
    xf = x.astype(jnp.float32)
    y = xf * lax.rsqrt(jnp.mean(xf * xf, axis=-1, keepdims=True) + EPS)
    return (y * g.astype(jnp.float32)).astype(x.dtype)


def layer_norm(x, g, b):
    xf = x.astype(jnp.float32)
    mu = jnp.mean(xf, axis=-1, keepdims=True)
    var = jnp.mean(jnp.square(xf - mu), axis=-1, keepdims=True)
    y = (xf - mu) * lax.rsqrt(var + EPS)
    return (y * g.astype(jnp.float32) + b.astype(jnp.float32)).astype(x.dtype)


def swiglu(h, w_gate, w_up, w_down):
    return (jax.nn.silu(h @ w_gate) * (h @ w_up)) @ w_down


def rope_cos_sin(positions):
    inv_freq = ROPE_THETA ** (-jnp.arange(0, ROT_DIM, 2, dtype=jnp.float32) / ROT_DIM)
    ang = positions.astype(jnp.float32)[..., None] * inv_freq
    return jnp.cos(ang), jnp.sin(ang)


def apply_partial_rope(x, cos, sin):
    cos = cos[:, :, None, None, :]
    sin = sin[:, :, None, None, :]
    xr = x[..., :ROT_DIM].astype(jnp.float32)
    xp = x[..., ROT_DIM:]
    x1, x2 = xr[..., :ROT_DIM // 2], xr[..., ROT_DIM // 2:]
    rot = jnp.concatenate([x1 * cos - x2 * sin, x2 * cos + x1 * sin], axis=-1)
    return jnp.concatenate([rot.astype(x.dtype), xp], axis=-1)


def causal_diff_attention(q, k, v, lam):
    b, s, h, _, d = q.shape
    nblk = s // Q_BLOCK
    scale = 1.0 / math.sqrt(d)
    qb = q.reshape(b, nblk, Q_BLOCK, h, 2, d).transpose(1, 0, 2, 3, 4, 5)
    kpos = jnp.arange(s)

    def one_block(args):
        qi, i = args
        sc = jnp.einsum('bqhcd,bkhcd->bhcqk', qi, k).astype(jnp.float32) * scale
        qpos = i * Q_BLOCK + jnp.arange(Q_BLOCK)
        mask = kpos[None, :] <= qpos[:, None]
        sc = jnp.where(mask, sc, -jnp.inf)
        p = jax.nn.softmax(sc, axis=-1)
        a = p[:, :, 0] - lam * p[:, :, 1]
        return jnp.einsum('bhqk,bkhe->bqhe', a.astype(v.dtype), v)

    out = lax.map(one_block, (qb, jnp.arange(nblk)))
    return out.transpose(1, 0, 2, 3, 4).reshape(b, s, h, v.shape[-1])


def conformer_conv_group(val, gate, conv_w, conv_b, ln_g, ln_b):
    u = val * jax.nn.sigmoid(gate)
    y = lax.conv_general_dilated(
        u, conv_w[:, None, :],
        window_strides=(1,), padding=[(CONV_K - 1, 0)],
        dimension_numbers=('NWC', 'WIO', 'NWC'),
        feature_group_count=CONV_WIDTH)
    y = y + conv_b
    return jax.nn.silu(layer_norm(y, ln_g, ln_b))


def memory_cross_attention(h, mem_h, w_q, w_kv, q_g, k_g, w_o):
    b, s, _ = h.shape
    q = rms_norm((h @ w_q).reshape(b, s, XATTN_HEADS, XATTN_HDIM), q_g)
    kv = mem_h @ w_kv
    k = rms_norm(kv[..., :XATTN_WIDTH].reshape(b, N_MEM, XATTN_HEADS, XATTN_HDIM), k_g)
    v = kv[..., XATTN_WIDTH:].reshape(b, N_MEM, XATTN_HEADS, XATTN_HDIM)
    sc = jnp.einsum('bqhd,bkhd->bhqk', q, k).astype(jnp.float32) * (1.0 / math.sqrt(XATTN_HDIM))
    p = jax.nn.softmax(sc, axis=-1)
    o = jnp.einsum('bhqk,bkhd->bqhd', p.astype(v.dtype), v).reshape(b, s, XATTN_WIDTH)
    return o @ w_o


def setup_inputs(seed: int = 0) -> dict:
    key = jax.random.key(seed)
    ks = iter(jax.random.split(key, 40))

    def nrm(shape, scale):
        return jax.random.normal(next(ks), shape, jnp.float32) * scale

    def gain(shape):
        return 1.0 + 0.02 * jax.random.normal(next(ks), shape, jnp.float32)

    L = DEPTH
    return {
        "x": nrm((BATCH, SEQ, D_MODEL), 1.0),
        "mem": nrm((BATCH, N_MEM, D_MODEL), 1.0),
        "positions": jnp.broadcast_to(jnp.arange(SEQ, dtype=jnp.int32), (BATCH, SEQ)),
        "ffn1_norm_g": gain((L, D_MODEL)),
        "ffn1_w_gate": nrm((L, D_MODEL, D_FF), D_MODEL ** -0.5),
        "ffn1_w_up": nrm((L, D_MODEL, D_FF), D_MODEL ** -0.5),
        "ffn1_w_down": nrm((L, D_FF, D_MODEL), D_FF ** -0.5),
        "mix_norm_g": gain((L, D_MODEL)),
        "w_in": nrm((L, D_MODEL, IN_COLS), D_MODEL ** -0.5),
        "q_norm_g": gain((L, DIFF_QKDIM)),
        "k_norm_g": gain((L, DIFF_QKDIM)),
        "lambda_q1": nrm((L, DIFF_QKDIM), 0.1),
        "lambda_k1": nrm((L, DIFF_QKDIM), 0.1),
        "lambda_q2": nrm((L, DIFF_QKDIM), 0.1),
        "lambda_k2": nrm((L, DIFF_QKDIM), 0.1),
        "diff_head_norm_g": gain((L, DIFF_VDIM)),
        "conv_w": nrm((L, CONV_K, CONV_WIDTH), CONV_K ** -0.5),
        "conv_b": nrm((L, CONV_WIDTH), 0.02),
        "conv_ln_g": gain((L, CONV_WIDTH)),
        "conv_ln_b": nrm((L, CONV_WIDTH), 0.02),
        "w_out": nrm((L, MIX_WIDTH, D_MODEL), MIX_WIDTH ** -0.5),
        "xattn_norm_g": gain((L, D_MODEL)),
        "mem_norm_g": gain((L, D_MODEL)),
        "xattn_w_q": nrm((L, D_MODEL, XATTN_WIDTH), D_MODEL ** -0.5),
        "xattn_w_kv": nrm((L, D_MODEL, 2 * XATTN_WIDTH), D_MODEL ** -0.5),
        "xattn_q_norm_g": gain((L, XATTN_HDIM)),
        "xattn_k_norm_g": gain((L, XATTN_HDIM)),
        "xattn_w_o": nrm((L, XATTN_WIDTH, D_MODEL), XATTN_WIDTH ** -0.5),
        "ffn2_norm_g": gain((L, D_MODEL)),
        "ffn2_w_gate": nrm((L, D_MODEL, D_FF), D_MODEL ** -0.5),
        "ffn2_w_up": nrm((L, D_MODEL, D_FF), D_MODEL ** -0.5),
        "ffn2_w_down": nrm((L, D_FF, D_MODEL), D_FF ** -0.5),
    }


def reference(x, mem, positions,
              ffn1_norm_g, ffn1_w_gate, ffn1_w_up, ffn1_w_down,
              mix_norm_g, w_in, q_norm_g, k_norm_g,
              lambda_q1, lambda_k1, lambda_q2, lambda_k2, diff_head_norm_g,
              conv_w, conv_b, conv_ln_g, conv_ln_b, w_out,
              xattn_norm_g, mem_norm_g, xattn_w_q, xattn_w_kv,
              xattn_q_norm_g, xattn_k_norm_g, xattn_w_o,
              ffn2_norm_g, ffn2_w_gate, ffn2_w_up, ffn2_w_down):
    b, s, _ = x.shape
    cos, sin = rope_cos_sin(positions)

    for l in range(DEPTH):
        x = x + 0.5 * swiglu(rms_norm(x, ffn1_norm_g[l]), ffn1_w_gate[l], ffn1_w_up[l], ffn1_w_down[l])

        h = rms_norm(x, mix_norm_g[l])
        p = h @ w_in[l]
        q = p[..., :DIFF_WIDTH].reshape(b, s, DIFF_HEADS, 2, DIFF_QKDIM)
        k = p[..., DIFF_WIDTH:2 * DIFF_WIDTH].reshape(b, s, DIFF_HEADS, 2, DIFF_QKDIM)
        v = p[..., 2 * DIFF_WIDTH:3 * DIFF_WIDTH].reshape(b, s, DIFF_HEADS, DIFF_VDIM)
        c_val = p[..., 3 * DIFF_WIDTH:3 * DIFF_WIDTH + CONV_WIDTH]
        c_gate = p[..., 3 * DIFF_WIDTH + CONV_WIDTH:]

        q = apply_partial_rope(rms_norm(q, q_norm_g[l]), cos, sin)
        k = apply_partial_rope(rms_norm(k, k_norm_g[l]), cos, sin)
        lambda_init = 0.8 - 0.6 * math.exp(-0.3 * l)
        lam = (jnp.exp(jnp.sum(lambda_q1[l].astype(jnp.float32) * lambda_k1[l].astype(jnp.float32)))
               - jnp.exp(jnp.sum(lambda_q2[l].astype(jnp.float32) * lambda_k2[l].astype(jnp.float32)))
               + lambda_init)
        o_diff = causal_diff_attention(q, k, v, lam)
        o_diff = rms_norm(o_diff, diff_head_norm_g[l]) * (1.0 - lambda_init)
        o_diff = o_diff.reshape(b, s, DIFF_WIDTH)

        o_conv = conformer_conv_group(c_val, c_gate, conv_w[l], conv_b[l], conv_ln_g[l], conv_ln_b[l])

        x = x + jnp.concatenate([o_diff, o_conv], axis=-1) @ w_out[l]

        x = x + memory_cross_attention(rms_norm(x, xattn_norm_g[l]), rms_norm(mem, mem_norm_g[l]),
                                       xattn_w_q[l], xattn_w_kv[l],
                                       xattn_q_norm_g[l], xattn_k_norm_g[l], xattn_w_o[l])

        x = x + 0.5 * swiglu(rms_norm(x, ffn2_norm_g[l]), ffn2_w_gate[l], ffn2_w_up[l], ffn2_w_down[l])
    return x
```

```python
import contextlib
import numpy as np
import concourse.bass as bass
import concourse.mybir as mybir
from concourse.bass_utils import run_bass_kernel_spmd

F32 = mybir.dt.float32
BF16 = mybir.dt.bfloat16
I32 = mybir.dt.int32
AF = mybir.ActivationFunctionType
ALU = mybir.AluOpType

NCORES = 8
D = 2048
T = 1024
S = 8192
DFF = 5632
NFC = DFF // 128
NW = 8
EPS = 1e-6

VC_G1, VC_GM, VC_GX, VC_GMEM, VC_G2 = 0, 16, 32, 48, 64
VC_QG, VC_KG = 80, 81
VC_LAM = 82
VC_DHG = 86
VC_CW = 87
VC_CB = VC_CW + 248
VC_LNG = VC_CB + 8
VC_LNB = VC_LNG + 8
VC_XQG = VC_LNB + 8
VC_XKG = VC_XQG + 1
VC_INVF = VC_XKG + 1
VC_HM = VC_INVF + 1
VC_EPS = VC_HM + 1
NVEC = 384


class Res:
    __slots__ = ("w", "r", "excl")

    def __init__(self, excl=False):
        self.w = None
        self.r = []
        self.excl = excl


class Prog:
    ENG = ("pe", "act", "dve", "pool", "sp")

    def __init__(self, nc, stack):
        self.nc = nc
        self.stack = stack
        self.q = {e: [] for e in self.ENG}
        self.semh = {}
        self.cnt = {}
        for e in self.ENG:
            self.newsem("c_" + e)

    def newsem(self, name):
        self.semh[name] = self.stack.enter_context(self.nc.semaphore(name))
        self.cnt[name] = 0
        return name

    def _deps(self, reads, writes, deps):
        toks = set(deps)
        for r in reads:
            if r.w is not None:
                toks.add(r.w)
            if r.excl:
                toks.update(r.r)
        for w in writes:
            if w.w is not None:
                toks.add(w.w)
            toks.update(w.r)
        return toks

    def _commit(self, tok, reads, writes):
        for r in reads:
            r.r.append(tok)
        for w in writes:
            w.w = tok
            w.r = []

    def op(self, eng, fn, reads=(), writes=(), deps=()):
        toks = self._deps(reads, writes, deps)
        key = "c_" + eng
        self.cnt[key] += 1
        tok = (key, self.cnt[key])
        self.q[eng].append((fn, toks, key, 1))
        self._commit(tok, reads, writes)
        return tok

    def wait(self, eng, deps):
        self.q[eng].append((None, set(deps), None, 0))

    def dma(self, eng, fns, sem, reads=(), writes=(), deps=(), inc=16):
        toks = self._deps(reads, writes, deps)
        first = True
        for fn in fns:
            self.cnt[sem] += inc
            self.q[eng].append((fn, toks if first else set(), sem, inc))
            first = False
        tok = (sem, self.cnt[sem])
        self._commit(tok, reads, writes)
        return tok

    def run(self, block):
        def mk(engname):
            def body(e):
                waited = {}
                own = "c_" + engname
                for fn, toks, sem, inc in self.q[engname]:
                    need = {}
                    for (k, v) in toks:
                        if v > need.get(k, 0):
                            need[k] = v
                    for k, v in need.items():
                        if k == own and engname == "pe":
                            continue
                        if waited.get(k, 0) < v:
                            e.wait_ge(self.semh[k], v)
                            waited[k] = v
                    if fn is None:
                        continue
                    ins = fn(e)
                    if sem is not None:
                        if inc == 1 and sem.startswith("cc_"):
                            ins.then_inc(self.semh[sem])
                        else:
                            ins.then_inc(self.semh[sem], inc)
            return body
        block.tensor(mk("pe"))
        block.vector(mk("dve"))
        block.scalar(mk("act"))
        block.gpsimd(mk("pool"))
        block.sync(mk("sp"))


SKIP_FFN = False
DEBUG = False


def plan_tiles():
    plan = []
    def ffn(pref):
        pend = None
        for g in range(NFC // 4):
            for jj in range(4):
                j = 4 * g + jj
                plan.append(("cols", pref + "_w_gate", j))
                plan.append(("cols", pref + "_w_up", j))
                if jj == 0 and pend is not None:
                    for q in range(4):
                        plan.append(("rows", pref + "_w_down", 4 * pend + q))
                    pend = None
            pend = g
        for q in range(4):
            plan.append(("rows", pref + "_w_down", 4 * pend + q))
    if not SKIP_FFN:
        ffn("ffn1")
    for j in range(24):
        plan.append(("cols", "w_in", j))
    for i in range(8):
        plan.append(("cols", "w_in", 24 + i))
        plan.append(("cols", "w_in", 32 + i))
    for pr in range(8):
        plan.append(("wout", 1, pr))
    for pr in range(8):
        plan.append(("wout", 0, pr))
    for j in range(8):
        plan.append(("cols", "xattn_w_kv", j))
    for j in range(4):
        plan.append(("cols", "xattn_w_q", j))
    for qq in range(4):
        plan.append(("wo", qq))
    if not SKIP_FFN:
        ffn("ffn2")
    return plan


def build(stage=99):
    nc = bass.Bass("TRN2", target_bir_lowering=False)
    plan = plan_tiles()
    NT = len(plan)

    def dram(name, shape, dtype, **kw):
        return nc.dram_tensor(name, shape, dtype, **kw).ap()

    xT = dram("xT", [D, T], F32, kind="ExternalInput")
    outT = dram("outT", [D, T], F32, kind="ExternalOutput")
    wall = dram("wall", [NT, 128, 2048], F32, kind="ExternalInput")
    vecs_d = dram("vecs", [128, NVEC], F32, kind="ExternalInput")
    cst_d = dram("cst", [128, 384 + 128 + 2048], F32, kind="ExternalInput")
    posr_d = dram("posr", [128, T], I32, kind="ExternalInput")
    memT_d = dram("memT", [D, 256], F32, kind="ExternalInput")
    idxA_d = dram("idxA", [128, 24], I32, kind="ExternalInput")
    idxB_d = dram("idxB", [128, 8], I32, kind="ExternalInput")
    idxH_d = dram("idxH", [128, 8], I32, kind="ExternalInput")
    ag1_src = dram("ag1_src", [3072, T], BF16)
    ag1_dst = dram("ag1_dst", [8 * 3072, T], BF16, addr_space="Shared")
    agh_src = dram("agh_src", [1024, 32], F32)
    agh_dst = dram("agh_dst", [8 * 1024, 32], F32, addr_space="Shared")
    ag2_src = dram("ag2_src", [1024, T], BF16)
    ag2_dst = dram("ag2_dst", [8 * 1024, T], BF16, addr_space="Shared")

    dbg_d = dram("dbg", [128, 49152], F32, kind="ExternalOutput") if DEBUG else None

    base = 16512
    OFF_X = base
    OFF_R1 = OFF_X + 65536
    OFF_WR = OFF_R1 + 32768
    OFF_R2 = OFF_WR + 32768
    OFF_R3 = OFF_R2 + 49152
    OFF_C = OFF_R3 + 16384

    def sb(name, shape, dtype, off):
        return nc.alloc_sbuf_tensor_at(name, shape, dtype, offset=off)

    stack = contextlib.ExitStack()
    with stack:
        P = Prog(nc, stack)
        X = sb("X", [128, 16, T], F32, OFF_X)
        H = sb("H", [128, 16, T], BF16, OFF_R1)
        SQ = sb("SQ", [128, 16, 512], F32, OFF_R1)
        WR = [sb("WR%d" % i, [128, 16, 128], BF16, OFF_WR + 4096 * i) for i in range(NW)]
        vecs = sb("vecs_s", [128, NVEC], F32, OFF_C)
        ones_f = sb("ones_f", [128, 128], F32, OFF_C + 1536)
        blk64 = sb("blk64", [128, 128], F32, OFF_C + 2048)
        prot = sb("prot", [128, 128], F32, OFF_C + 2560)
        ident = sb("ident", [128, 128], BF16, OFF_C + 3072)
        masks = sb("masks", [128, 4, 512], BF16, OFF_C + 3328)

        B = [stack.enter_context(nc.psum_tensor("bank%d" % i, [128, 512], F32)) for i in range(8)]
        Br = [Res(excl=True) for _ in range(8)]

        Xr = [[Res(), Res()] for _ in range(16)]
        Hr = [[Res(), Res()] for _ in range(16)]
        Cr = Res()

        s_x = P.newsem("d_x")
        tok = P.dma("sp", [(lambda e, c=c: e.dma_start(out=X[:, c, :], in_=xT[c * 128:(c + 1) * 128, :])) for c in range(16)],
                    s_x, writes=[Xr[c][h] for c in range(16) for h in range(2)])
        s_c = P.newsem("d_c")
        P.dma("sp", [lambda e: e.dma_start(out=vecs[:, :], in_=vecs_d[:, :]),
                     lambda e: e.dma_start(out=ones_f[:, :], in_=cst_d[:, 0:128]),
                     lambda e: e.dma_start(out=blk64[:, :], in_=cst_d[:, 128:256]),
                     lambda e: e.dma_start(out=prot[:, :], in_=cst_d[:, 256:384])], s_c, writes=[Cr])
        s_c2 = P.newsem("d_c2")
        Cr2 = Res()
        P.dma("pool", [lambda e: e.dma_start(out=ident[:, :], in_=cst_d[:, 384:512]),
                       lambda e: e.dma_start(out=masks[:, :, :], in_=cst_d[:, 512:2560].rearrange("p (j q) -> p j q", j=4))],
              s_c2, writes=[Cr2])

        class WS:
            def __init__(self):
                self.k = 0
                self.res = [Res() for _ in range(NW)]
                self.sems = [P.newsem("d_w%d" % i) for i in range(NW)]
                self.loaded = 0
                for _ in range(NW):
                    self._load()

            def _load(self):
                k = self.loaded
                if k >= NT:
                    return
                s = k % NW
                P.dma("pool", [lambda e, k=k, s=s: e.dma_start(out=WR[s][:, :, :], in_=wall[k].rearrange("p (c f) -> p c f", c=16))],
                      self.sems[s], writes=[self.res[s]])
                self.loaded += 1

            def next(self, spec):
                assert plan[self.k] == spec, (self.k, plan[self.k], spec)
                assert self.k < self.loaded
                s = self.k % NW
                self.k += 1
                return WR[s], self.res[s]

            def release(self):
                self._load()

        ws = WS()

        def sl(half):
            return slice(half * 512, half * 512 + 512)

        def rmsnorm(gcol, r2off):
            sd = [sb("sd%d_%d" % (gcol, h), [128, 512], F32, OFF_R2 + r2off + 2048 * h) for h in range(2)]
            rs = [sb("rs%d_%d" % (gcol, h), [128, 512], F32, OFF_R2 + r2off + 4096 + 2048 * h) for h in range(2)]
            sdr = [Res(), Res()]
            rsr = [Res(), Res()]
            for half in range(2):
                hs = sl(half)
                for c in range(16):
                    P.op("act", lambda e, c=c, hs=hs: e.activation(out=SQ[:, c, :], in_=X[:, c, hs], func=AF.Square),
                         reads=[Xr[c][half]], writes=[Hr[c][0], Hr[c][1]])

                def stat(e):
                    ins = None
                    for c in range(16):
                        ins = e.matmul(B[7][:, :], lhsT=ones_f[:, :], rhs=SQ[:, c, :], start=(c == 0), stop=(c == 15))
                    return ins
                P.op("pe", stat, reads=[Hr[c][h] for c in range(16) for h in range(2)] + [Cr], writes=[Br[7]])
                P.op("act", lambda e, half=half: e.activation(out=sd[half][:, :], in_=B[7][:, :], func=AF.Sqrt,
                                                              bias=vecs[:, VC_EPS:VC_EPS + 1], scale=1.0 / D),
                     reads=[Br[7], Cr], writes=[sdr[half]])
                P.op("dve", lambda e, half=half: e.reciprocal(out=rs[half][:, :], in_=sd[half][:, :]),
                     reads=[sdr[half]], writes=[rsr[half]])
            for half in range(2):
                hs = sl(half)
                for c in range(16):
                    P.op("dve", lambda e, c=c, hs=hs, half=half: e.scalar_tensor_tensor(
                        out=H[:, c, hs], in0=X[:, c, hs], scalar=vecs[:, gcol + c:gcol + c + 1], in1=rs[half][:, :],
                        op0=ALU.mult, op1=ALU.mult),
                        reads=[Xr[c][half], rsr[half], Cr], writes=[Hr[c][half]])

        def ffn(pref, gcol):
            rmsnorm(gcol, 40960)
            AT = [[sb("AT_%s_%d_%d" % (pref, a, b), [128, T], BF16, OFF_R2 + (a * 4 + b) * 2048) for b in range(4)] for a in range(2)]
            ATr = [[[Res(), Res()] for b in range(4)] for a in range(2)]
            ST = [sb("ST_%s_%d" % (pref, a), [128, 512], F32, OFF_R2 + 16384 + 2048 * a) for a in range(2)]
            STr = [Res(), Res()]
            dcnt = [0]

            def emit_down(g):
                wd = [ws.next(("rows", pref + "_w_down", 4 * g + q)) for q in range(4)]
                for i in range(16):
                    for half in range(2):
                        hs = sl(half)
                        bk = 4 + dcnt[0] % 3
                        dcnt[0] += 1

                        def mm(e, i=i, hs=hs, bk=bk, g=g):
                            ins = None
                            for q in range(4):
                                ins = e.matmul(B[bk][:, :], lhsT=wd[q][0][:, i, :], rhs=AT[g % 2][q][:, hs],
                                               start=(q == 0), stop=(q == 3))
                            return ins
                        P.op("pe", mm, reads=[wd[q][1] for q in range(4)] + [ATr[g % 2][q][half] for q in range(4)],
                             writes=[Br[bk]])
                        P.op("dve", lambda e, i=i, hs=hs, bk=bk: e.scalar_tensor_tensor(
                            out=X[:, i, hs], in0=B[bk][:, :], scalar=0.5, in1=X[:, i, hs], op0=ALU.mult, op1=ALU.add),
                            reads=[Br[bk]], writes=[Xr[i][half]])
                for q in range(4):
                    ws.release()

            pend = None
            for g in range(NFC // 4):
                for jj in range(4):
                    j = 4 * g + jj
                    wg, wgr = ws.next(("cols", pref + "_w_gate", j))
                    wu, wur = ws.next(("cols", pref + "_w_up", j))
                    for half in range(2):
                        hs = sl(half)
                        par = half
                        bg, bu = 2 * par, 2 * par + 1

                        def mmg(e, w=wg, hs=hs, bk=bg):
                            ins = None
                            for c in range(16):
                                ins = e.matmul(B[bk][:, :], lhsT=w[:, c, :], rhs=H[:, c, hs], start=(c == 0), stop=(c == 15))
                            return ins
                        P.op("pe", mmg, reads=[wgr] + [Hr[c][half] for c in range(16)], writes=[Br[bg]])

                        def mmu(e, w=wu, hs=hs, bk=bu):
                            ins = None
                            for c in range(16):
                                ins = e.matmul(B[bk][:, :], lhsT=w[:, c, :], rhs=H[:, c, hs], start=(c == 0), stop=(c == 15))
                            return ins
                        P.op("pe", mmu, reads=[wur] + [Hr[c][half] for c in range(16)], writes=[Br[bu]])
                        P.op("act", lambda e, par=par, bk=bg: e.activation(out=ST[par][:, :], in_=B[bk][:, :], func=AF.Silu),
                             reads=[Br[bg]], writes=[STr[par]])
                        P.op("dve", lambda e, par=par, bk=bu, g=g, jj=jj, hs=hs: e.tensor_tensor(
                            out=AT[g % 2][jj][:, hs], in0=ST[par][:, :], in1=B[bk][:, :], op=ALU.mult),
                            reads=[STr[par], Br[bu]], writes=[ATr[g % 2][jj][half]])
                    ws.release()
                    ws.release()
                    if jj == 0 and pend is not None:
                        emit_down(pend)
                        pend = None
                pend = g
            emit_down(pend)

        dbg_toks = []

        def tap(col0, ap2d, n, reads):
            if not DEBUG:
                return
            sname = P.newsem("d_dbg%d" % col0)
            dbg_toks.append(P.dma("pool", [(lambda e, j=j: e.dma_start(out=dbg_d[:, col0 + j * 1024:col0 + (j + 1) * 1024],
                                                                        in_=ap2d[:, j * 1024:(j + 1) * 1024])) for j in range(n // 1024)],
                                  sname, reads=reads))

        def proj_group(bk, w, wr, half, extra_reads=()):
            hs = sl(half)

            def mm(e):
                ins = None
                for c in range(16):
                    ins = e.matmul(B[bk][:, :], lhsT=w[:, c, :], rhs=H[:, c, hs], start=(c == 0), stop=(c == 15))
                return ins
            return P.op("pe", mm, reads=[wr] + [Hr[c][half] for c in range(16)] + list(extra_reads), writes=[Br[bk]])

        def R2(name, shape, dtype, off):
            return sb(name, shape, dtype, OFF_R2 + off)

        def R3(name, shape, dtype, off):
            return sb(name, shape, dtype, OFF_R3 + off)

        def mixing():
            ones_b = sb("ones_b", [128, 128], BF16, OFF_C + 7424)
            s_ob = P.newsem("d_ob")
            obr = Res()
            P.dma("pool", [lambda e: e.dma_start(out=ones_b[:, :], in_=cst_d[:, 0:128])], s_ob, writes=[obr])

            Ct = R3("Ct", [128, T], F32, 0)
            St = R3("St", [128, T], F32, 4096)
            ang = R3("ang", [128, T], F32, 8192)
            tmpf = R3("tmpf", [128, T], F32, 12288)
            tmpi = R3("tmpi", [128, T], I32, 12288)
            rt = Res()
            s_p = P.newsem("d_pos")
            P.dma("sp", [lambda e: e.dma_start(out=tmpi[:, :], in_=posr_d[:, :])], s_p, writes=[rt])
            TWO_PI = 6.283185307179586
            C1 = 6.28125
            C2 = TWO_PI - C1
            PI = 3.141592653589793

            def dv(fn):
                P.op("dve", fn, reads=[rt, Cr], writes=[rt])
            dv(lambda e: e.tensor_copy(out=ang[:, :], in_=tmpi[:, :]))
            dv(lambda e: e.tensor_scalar(out=ang[:, :], in0=ang[:, :], scalar1=vecs[:, VC_INVF:VC_INVF + 1], scalar2=None, op0=ALU.mult))
            dv(lambda e: e.tensor_scalar(out=tmpf[:, :], in0=ang[:, :], scalar1=1.0 / TWO_PI, scalar2=None, op0=ALU.mult))
            kint = R3("kint", [128, T], I32, 0)
            kf = R3("kf", [128, T], F32, 4096)
            dv(lambda e: e.tensor_copy(out=kint[:, :], in_=tmpf[:, :]))
            dv(lambda e: e.tensor_copy(out=kf[:, :], in_=kint[:, :]))
            dv(lambda e: e.scalar_tensor_tensor(out=ang[:, :], in0=kf[:, :], scalar=-C1, in1=ang[:, :], op0=ALU.mult, op1=ALU.add))
            dv(lambda e: e.scalar_tensor_tensor(out=ang[:, :], in0=kf[:, :], scalar=-C2, in1=ang[:, :], op0=ALU.mult, op1=ALU.add))

            def fold(u):
                dv(lambda e: e.tensor_scalar(out=tmpf[:, :], in0=u[:, :], scalar1=PI, scalar2=None, op0=ALU.is_gt))
                dv(lambda e: e.scalar_tensor_tensor(out=u[:, :], in0=tmpf[:, :], scalar=-TWO_PI, in1=u[:, :], op0=ALU.mult, op1=ALU.add))
                dv(lambda e: e.tensor_scalar(out=tmpf[:, :], in0=u[:, :], scalar1=-PI, scalar2=None, op0=ALU.is_lt))
                dv(lambda e: e.scalar_tensor_tensor(out=u[:, :], in0=tmpf[:, :], scalar=TWO_PI, in1=u[:, :], op0=ALU.mult, op1=ALU.add))
            fold(ang)
            P.op("act", lambda e: e.activation(out=St[:, :], in_=ang[:, :], func=AF.Sin), reads=[rt], writes=[rt])
            dv(lambda e: e.tensor_scalar(out=ang[:, :], in0=ang[:, :], scalar1=PI / 2, scalar2=None, op0=ALU.add))
            fold(ang)
            P.op("act", lambda e: e.activation(out=Ct[:, :], in_=ang[:, :], func=AF.Sin), reads=[rt], writes=[rt])

            lamt = sb("lamt", [128, 8], F32, OFF_C + 7680)
            lr = Res()
            P.op("dve", lambda e: e.tensor_tensor(out=lamt[0:64, 0:1], in0=vecs[0:64, VC_LAM:VC_LAM + 1], in1=vecs[0:64, VC_LAM + 1:VC_LAM + 2], op=ALU.mult),
                 reads=[Cr], writes=[lr])
            P.op("dve", lambda e: e.tensor_tensor(out=lamt[0:64, 1:2], in0=vecs[0:64, VC_LAM + 2:VC_LAM + 3], in1=vecs[0:64, VC_LAM + 3:VC_LAM + 4], op=ALU.mult),
                 reads=[Cr], writes=[lr])
            P.op("pe", lambda e: e.matmul(B[7][:, 0:2], lhsT=ones_f[0:64, :], rhs=lamt[0:64, 0:2], start=True, stop=True),
                 reads=[lr, Cr], writes=[Br[7]])
            P.op("act", lambda e: e.activation(out=lamt[:, 2:4], in_=B[7][:, 0:2], func=AF.Exp), reads=[Br[7]], writes=[lr])
            P.op("dve", lambda e: e.tensor_tensor(out=lamt[:, 4:5], in0=lamt[:, 2:3], in1=lamt[:, 3:4], op=ALU.subtract), reads=[lr], writes=[lr])
            P.op("dve", lambda e: e.tensor_scalar(out=lamt[:, 5:6], in0=lamt[:, 4:5], scalar1=0.2, scalar2=-1.0, op0=ALU.add, op1=ALU.mult),
                 reads=[lr], writes=[lr])
            P.op("dve", lambda e: e.tensor_scalar(out=lamt[:, 6:7], in0=vecs[:, VC_DHG:VC_DHG + 1], scalar1=0.8, scalar2=None, op0=ALU.mult),
                 reads=[lr, Cr], writes=[lr])
            nlam = lamt[:, 5:6]
            dhg08 = lamt[:, 6:7]

            rmsnorm(VC_GM, 40960)

            stg = [[R2("stg%d_%d" % (a, b), [128, 512], F32, (a * 6 + b) * 2048) for b in range(6)] for a in range(2)]
            stgr = [[Res() for b in range(6)] for a in range(2)]
            QT = [R2("QTs%d" % a, [128, T], BF16, 24576 + 2048 * a) for a in range(2)]
            QTr = [[Res(), Res()] for a in range(2)]
            s_st = P.newsem("d_st1")
            store_toks = []
            n = 0
            for wh in range(3):
                for h in range(8):
                    w, wr = ws.next(("cols", "w_in", wh * 8 + h))
                    qp = n % 2
                    n += 1
                    for half in range(2):
                        hs = sl(half)
                        proj_group(half, w, wr, half)
                        if wh == 2:
                            P.op("act", lambda e, qp=qp, hs=hs, half=half: e.activation(out=QT[qp][:, hs], in_=B[half][:, :], func=AF.Copy),
                                 reads=[Br[half]], writes=[QTr[qp][half]])
                            continue
                        sq, qg, sdq, rsq, t1, t2 = stg[half]
                        sqr, qgr, sdqr, rsqr, t1r, t2r = stgr[half]
                        gcol = VC_QG if wh == 0 else VC_KG
                        P.op("act", lambda e, sq=sq, half=half: e.activation(out=sq[:, :], in_=B[half][:, :], func=AF.Square),
                             reads=[Br[half]], writes=[sqr])
                        P.op("dve", lambda e, qg=qg, half=half, gcol=gcol: e.tensor_scalar(
                            out=qg[:, :], in0=B[half][:, :], scalar1=vecs[:, gcol:gcol + 1], scalar2=None, op0=ALU.mult),
                            reads=[Br[half], Cr], writes=[qgr])
                        P.op("pe", lambda e, sq=sq, half=half: e.matmul(B[2 + half][:, :], lhsT=blk64[:, :], rhs=sq[:, :], start=True, stop=True),
                             reads=[sqr, Cr], writes=[Br[2 + half]])
                        P.op("pe", lambda e, qg=qg, half=half: e.matmul(B[4 + half][:, :], lhsT=prot[:, :], rhs=qg[:, :], start=True, stop=True),
                             reads=[qgr, Cr], writes=[Br[4 + half]])
                        P.op("act", lambda e, sdq=sdq, half=half: e.activation(out=sdq[:, :], in_=B[2 + half][:, :], func=AF.Sqrt,
                                                                          bias=vecs[:, VC_EPS:VC_EPS + 1], scale=1.0 / 64),
                             reads=[Br[2 + half], Cr], writes=[sdqr])
                        P.op("dve", lambda e, sdq=sdq, rsq=rsq: e.reciprocal(out=rsq[:, :], in_=sdq[:, :]), reads=[sdqr], writes=[rsqr])
                        P.op("dve", lambda e, t1=t1, qg=qg, hs=hs: e.tensor_tensor(out=t1[:, :], in0=qg[:, :], in1=Ct[:, hs], op=ALU.mult),
                             reads=[qgr, rt], writes=[t1r])
                        P.op("dve", lambda e, t2=t2, half=half, hs=hs: e.tensor_tensor(out=t2[:, :], in0=B[4 + half][:, :], in1=St[:, hs], op=ALU.mult),
                             reads=[Br[4 + half], rt], writes=[t2r])
                        P.op("dve", lambda e, t1=t1, t2=t2: e.tensor_tensor(out=t1[:, :], in0=t1[:, :], in1=t2[:, :], op=ALU.add),
                             reads=[t1r, t2r], writes=[t1r])
                        P.op("dve", lambda e, t1=t1, rsq=rsq, qp=qp, hs=hs: e.tensor_tensor(out=QT[qp][:, hs], in0=t1[:, :], in1=rsq[:, :], op=ALU.mult),
                             reads=[t1r, rsqr], writes=[QTr[qp][half]])
                    ws.release()
                    row0 = (wh * 8 + h) * 128
                    store_toks.append(P.dma("sp", [lambda e, qp=qp, row0=row0: e.dma_start(out=ag1_src[row0:row0 + 128, :], in_=QT[qp][:, :])],
                                            s_st, reads=[QTr[qp][0], QTr[qp][1]]))
            s_cc1 = P.newsem("cc_1")
            ag1r = Res()
            P.dma("pool", [lambda e: e.collective_compute("AllGather", ALU.bypass, replica_groups=[list(range(NCORES))],
                                                          ins=[ag1_src[:, :]], outs=[ag1_dst[:, :]])],
                  s_cc1, writes=[ag1r], deps=store_toks, inc=1)

            barrier_light = barrier
            barrier_light()

            idxH = sb("idxH_s", [128, 8], I32, OFF_C + 7712)
            idxA = sb("idxA_s", [128, 24], I32, OFF_C + 7744)
            idxB = sb("idxB_s", [128, 8], I32, OFF_C + 7840)
            s_ix = P.newsem("d_ix")
            ixr = Res()
            P.dma("sp", [lambda e: e.dma_start(out=idxH[:, :], in_=idxH_d[:, :]),
                         lambda e: e.dma_start(out=idxA[:, :], in_=idxA_d[:, :]),
                         lambda e: e.dma_start(out=idxB[:, :], in_=idxB_d[:, :])], s_ix, writes=[ixr])
            UY = R2("UY", [128, 8, 1054], F32, 0)
            UYr = [[Res(), Res(), Res()] for _ in range(8)]
            sg = [R2("sg%d" % a, [128, 512], F32, 33792 + 2048 * a) for a in range(2)]
            sgr = [Res(), Res()]
            tmpc = [R2("tmpc%d" % a, [128, T], F32, 37888 + 4096 * a) for a in range(2)]
            tmpcr = [Res(), Res()]
            HL = R2("HL", [128, 8, 32], F32, 46080)
            HLr = Res()
            s_th = P.newsem("d_tail")
            tail_toks = []
            for i in range(8):
                wv, wvr = ws.next(("cols", "w_in", 24 + i))
                wg_, wgr_ = ws.next(("cols", "w_in", 32 + i))
                for half in range(2):
                    hs = sl(half)
                    proj_group(half, wv, wvr, half)
                    proj_group(2 + half, wg_, wgr_, half)
                    P.op("act", lambda e, half=half: e.activation(out=sg[half][:, :], in_=B[2 + half][:, :], func=AF.Sigmoid),
                         reads=[Br[2 + half]], writes=[sgr[half]])
                    P.op("dve", lambda e, half=half, i=i: e.tensor_tensor(out=UY[:, i, 30 + half * 512:30 + half * 512 + 512],
                                                                       in0=B[half][:, :], in1=sg[half][:, :], op=ALU.mult),
                         reads=[Br[half], sgr[half]], writes=[UYr[i][half]])
                ws.release()
                ws.release()
                tail_toks.append(P.dma("sp", [lambda e, i=i: e.dma_start(out=agh_src[i * 128:(i + 1) * 128, :], in_=UY[:, i, 1022:1054])],
                                       s_th, reads=[UYr[i][1]]))
            s_cch = P.newsem("cc_h")
            aghr = Res()
            P.dma("pool", [lambda e: e.collective_compute("AllGather", ALU.bypass, replica_groups=[list(range(NCORES))],
                                                          ins=[agh_src[:, :]], outs=[agh_dst[:, :]])],
                  s_cch, writes=[aghr], deps=tail_toks, inc=1)
            s_gh = P.newsem("d_gh")
            P.dma("pool", [(lambda e, i=i: e.indirect_dma_start(out=HL[:, i, :], out_offset=None, in_=agh_dst[:, :],
                                                                in_offset=bass.IndirectOffsetOnAxis(ap=idxH[:, i:i + 1], axis=0)))
                           for i in range(8)], s_gh, reads=[aghr, ixr], writes=[HLr])
            for i in range(8):
                P.op("dve", lambda e, i=i: e.tensor_scalar(out=UY[:, i, 0:30], in0=HL[:, i, 2:32], scalar1=vecs[:, VC_HM:VC_HM + 1],
                                                          scalar2=None, op0=ALU.mult),
                     reads=[HLr, Cr], writes=[UYr[i][2]])
            for i in range(8):
                tp = tmpc[i % 2]
                tpr = tmpcr[i % 2]
                rd = [UYr[i][0], UYr[i][1], UYr[i][2], Cr]
                wc = VC_CW + i * 31
                P.op("dve", lambda e, i=i, tp=tp, wc=wc: e.tensor_scalar(out=tp[:, :], in0=UY[:, i, 0:T], scalar1=vecs[:, wc:wc + 1],
                                                                       scalar2=vecs[:, VC_CB + i:VC_CB + i + 1], op0=ALU.mult, op1=ALU.add),
                     reads=rd, writes=[tpr])
                for k in range(1, 30):
                    P.op("dve", lambda e, i=i, tp=tp, wc=wc, k=k: e.scalar_tensor_tensor(
                        out=tp[:, :], in0=UY[:, i, k:k + T], scalar=vecs[:, wc + k:wc + k + 1], in1=tp[:, :], op0=ALU.mult, op1=ALU.add),
                        reads=rd + [tpr], writes=[tpr])
                P.op("dve", lambda e, i=i, tp=tp, wc=wc: e.scalar_tensor_tensor(
                    out=UY[:, i, 30:30 + T], in0=UY[:, i, 30:30 + T], scalar=vecs[:, wc + 30:wc + 31], in1=tp[:, :], op0=ALU.mult, op1=ALU.add),
                    reads=rd + [tpr], writes=[UYr[i][0], UYr[i][1]])
            OC = R3("OC", [128, 8, T], BF16, 0)
            OCr = [[Res(), Res()] for _ in range(8)]
            lnt = [R2("lnt%d" % a, [128, 512], F32, 37888 + 2048 * a) for a in range(4)]
            lntr = [Res() for _ in range(4)]
            for half in range(2):
                hs = sl(half)
                ys = slice(30 + half * 512, 30 + half * 512 + 512)

                def msum(e, ys=ys):
                    ins = None
                    for i in range(8):
                        ins = e.matmul(B[4][:, :], lhsT=ones_f[:, :], rhs=UY[:, i, ys], start=(i == 0), stop=(i == 7))
                    return ins
                P.op("pe", msum, reads=[UYr[i][half] for i in range(8)] + [Cr], writes=[Br[4]])
                for i in range(8):
                    P.op("act", lambda e, i=i, ys=ys: e.activation(out=sg[i % 2][:, :], in_=UY[:, i, ys], func=AF.Square),
                         reads=[UYr[i][half]], writes=[sgr[i % 2]])
                    P.op("pe", lambda e, i=i: e.matmul(B[5][:, :], lhsT=ones_f[:, :], rhs=sg[i % 2][:, :], start=(i == 0), stop=(i == 7)),
                         reads=[sgr[i % 2], Cr], writes=[Br[5]])
                mean, m2, var, rstd = lnt
                P.op("dve", lambda e: e.tensor_scalar(out=mean[:, :], in0=B[4][:, :], scalar1=1.0 / 1024, scalar2=None, op0=ALU.mult),
                     reads=[Br[4]] + tmpcr, writes=[lntr[0]])
                P.op("dve", lambda e: e.tensor_tensor(out=m2[:, :], in0=mean[:, :], in1=mean[:, :], op=ALU.mult),
                     reads=[lntr[0]] + tmpcr, writes=[lntr[1]])
                P.op("dve", lambda e: e.scalar_tensor_tensor(out=var[:, :], in0=B[5][:, :], scalar=1.0 / 1024, in1=m2[:, :],
                                                             op0=ALU.mult, op1=ALU.subtract),
                     reads=[Br[5], lntr[1]] + tmpcr, writes=[lntr[2]])
                P.op("act", lambda e: e.activation(out=var[:, :], in_=var[:, :], func=AF.Sqrt, bias=vecs[:, VC_EPS:VC_EPS + 1], scale=1.0),
                     reads=[lntr[2], Cr], writes=[lntr[2]])
                P.op("dve", lambda e: e.reciprocal(out=rstd[:, :], in_=var[:, :]), reads=[lntr[2]] + tmpcr, writes=[lntr[3]])
                for i in range(8):
                    P.op("dve", lambda e, i=i, ys=ys: e.tensor_tensor(out=UY[:, i, ys], in0=UY[:, i, ys], in1=mean[:, :], op=ALU.subtract),
                         reads=[lntr[0]], writes=[UYr[i][half]])
                    P.op("dve", lambda e, i=i, ys=ys: e.tensor_tensor(out=UY[:, i, ys], in0=UY[:, i, ys], in1=rstd[:, :], op=ALU.mult),
                         reads=[lntr[3]], writes=[UYr[i][half]])
                    P.op("act", lambda e, i=i, ys=ys, hs=hs: e.activation(out=OC[:, i, hs], in_=UY[:, i, ys], func=AF.Silu,
                                                                        bias=vecs[:, VC_LNB + i:VC_LNB + i + 1],
                                                                        scale=vecs[:, VC_LNG + i:VC_LNG + i + 1]),
                         reads=[UYr[i][half], Cr, rt], writes=[OCr[i][half]])

            tap(0, OC[:, :, :].rearrange("p a b -> p (a b)"), 8192, [OCr[i][h] for i in range(8) for h in range(2)])
            def wout_half(hh, SRC, SRCr):
                cnt = 0
                for pr in range(8):
                    w, wr = ws.next(("wout", hh, pr))
                    for ii in range(2):
                        for half in range(2):
                            hs = sl(half)
                            bk = 6 + cnt % 2
                            cnt += 1

                            def mm(e, w=w, ii=ii, hs=hs, bk=bk):
                                ins = None
                                for kk in range(8):
                                    ins = e.matmul(B[bk][:, :], lhsT=w[:, kk * 2 + ii, :], rhs=SRC[:, kk, hs], start=(kk == 0), stop=(kk == 7))
                                return ins
                            P.op("pe", mm, reads=[wr] + [SRCr[kk][half] for kk in range(8)], writes=[Br[bk]])
                            i = 2 * pr + ii
                            P.op("dve", lambda e, i=i, hs=hs, bk=bk: e.tensor_tensor(out=X[:, i, hs], in0=B[bk][:, :], in1=X[:, i, hs], op=ALU.add),
                                 reads=[Br[bk]], writes=[Xr[i][half]])
                    ws.release()
            wout_half(1, OC, OCr)

            barrier()

            QTc = sb("QTc", [128, 8, T], BF16, OFF_R1)
            KTc = sb("KTc", [128, 8, T], BF16, OFF_R1 + 16384)
            qkr = Res()
            s_g1 = P.newsem("d_g1")
            fns = []
            for r in range(8):
                fns.append(lambda e, r=r: e.indirect_dma_start(out=QTc[:, r, :], out_offset=None, in_=ag1_dst[:, :],
                                                               in_offset=bass.IndirectOffsetOnAxis(ap=idxA[:, 3 * r:3 * r + 1], axis=0)))
                fns.append(lambda e, r=r: e.indirect_dma_start(out=KTc[:, r, :], out_offset=None, in_=ag1_dst[:, :],
                                                               in_offset=bass.IndirectOffsetOnAxis(ap=idxA[:, 3 * r + 1:3 * r + 2], axis=0)))
            P.dma("pool", fns, s_g1, reads=[ag1r, ixr], writes=[qkr] + [Hr[c][h] for c in range(16) for h in range(2)])
            Vtok = R2("Vtok", [128, 64, 128], BF16, 0)
            Vr = [Res() for _ in range(16)]
            VTs = [R2("VTs%d" % a, [128, T], BF16, 16384 + 2048 * a) for a in range(2)]
            VTsr = [Res(), Res()]
            s_gv = [P.newsem("d_gv%d" % a) for a in range(2)]
            for r in range(8):
                a = r % 2
                P.dma("pool", [lambda e, r=r, a=a: e.indirect_dma_start(out=VTs[a][:, :], out_offset=None, in_=ag1_dst[:, :],
                                                                        in_offset=bass.IndirectOffsetOnAxis(ap=idxA[:, 3 * r + 2:3 * r + 3], axis=0))],
                      s_gv[a], reads=[ag1r, ixr], writes=[VTsr[a]])
                for q4 in range(2):
                    bk = (2 * r + q4) % 2

                    def tr(e, a=a, q4=q4, bk=bk):
                        ins = None
                        for tb in range(4):
                            t0 = (q4 * 4 + tb) * 128
                            ins = e.matmul(B[bk][:, tb * 128:(tb + 1) * 128], lhsT=VTs[a][:, t0:t0 + 128], rhs=ident[:, :], start=True, stop=True)
                        return ins
                    P.op("pe", tr, reads=[VTsr[a], Cr2], writes=[Br[bk]])
                    g4 = 2 * r + q4
                    P.op("act", lambda e, g4=g4, bk=bk: e.activation(out=Vtok[:, g4 * 4:(g4 + 1) * 4, :],
                                                                     in_=B[bk][:, :].rearrange("p (a b) -> p a b", a=4), func=AF.Copy),
                         reads=[Br[bk]], writes=[Vr[g4]])

            tap(8192, QTc[:, :, :].rearrange("p a b -> p (a b)"), 8192, [qkr])
            tap(16384, KTc[:, :, :].rearrange("p a b -> p (a b)"), 8192, [qkr])
            tap(24576, Vtok[:, :, :].rearrange("p a b -> p (a b)"), 8192, Vr)
            PT = [R2("PT%d" % a, [128, 512], BF16, 20480 + 1024 * a) for a in range(6)]
            PTr = [Res() for _ in range(6)]
            et = [R2("et%d" % a, [128, 512], F32, 26624 + 2048 * a) for a in range(8)]
            etr = [Res() for _ in range(8)]
            OT = R3("OT", [128, S], BF16, 0)
            OTr = [Res() for _ in range(16)]
            SB_ = [0, 1, 2]
            scnt = 0
            pcnt = 0
            for qt in range(16):
                r_q, half = qt // 2, qt % 2
                hs = sl(half)
                nkb = 4 * (qt + 1)
                items = [(sub, kb) for kb in range(nkb) for sub in range(2)]
                LOOK = 2
                pend = []

                def emit_s(sub, kb):
                    nonlocal scnt, pcnt
                    bk = SB_[scnt % 3]
                    scnt += 1
                    pi = pcnt % 6
                    pcnt += 1
                    ps = slice(sub * 64, sub * 64 + 64)
                    P.op("pe", lambda e, bk=bk, ps=ps, kb=kb, r_q=r_q, hs=hs: e.matmul(
                        B[bk][:, :], lhsT=KTc[ps, kb // 8, (kb % 8) * 128:(kb % 8) * 128 + 128], rhs=QTc[ps, r_q, hs], start=True, stop=True),
                         reads=[qkr], writes=[Br[bk]])
                    P.op("act", lambda e, bk=bk, pi=pi: e.activation(out=PT[pi][:, :], in_=B[bk][:, :], func=AF.Exp, scale=0.125),
                         reads=[Br[bk]], writes=[PTr[pi]])
                    if kb >= 4 * qt:
                        j = kb - 4 * qt
                        P.op("dve", lambda e, pi=pi, j=j: e.tensor_tensor(out=PT[pi][:, :], in0=PT[pi][:, :], in1=masks[:, j, :], op=ALU.mult),
                             reads=[PTr[pi], Cr2], writes=[PTr[pi]])
                    return pi

                def emit_av(sub, kb, pi, first, last):
                    ob, lb = 3 + sub, 5 + sub
                    P.op("pe", lambda e, ob=ob, kb=kb, pi=pi, first=first, last=last: e.matmul(
                        B[ob][:, :], lhsT=Vtok[:, kb, :], rhs=PT[pi][:, :], start=first, stop=last),
                        reads=[Vr[kb // 4], PTr[pi]], writes=[Br[ob]])
                    P.op("pe", lambda e, lb=lb, pi=pi, first=first, last=last: e.matmul(
                        B[lb][:, :], lhsT=ones_b[:, :], rhs=PT[pi][:, :], start=first, stop=last),
                        reads=[obr, PTr[pi]], writes=[Br[lb]])

                for idx_, (sub, kb) in enumerate(items):
                    pi = emit_s(sub, kb)
                    pend.append((sub, kb, pi))
                    if len(pend) > LOOK:
                        s_, k_, p_ = pend.pop(0)
                        emit_av(s_, k_, p_, k_ == 0, k_ == nkb - 1)
                while pend:
                    s_, k_, p_ = pend.pop(0)
                    emit_av(s_, k_, p_, k_ == 0, k_ == nkb - 1)
                r1, r2, a_, b_, o_, osq, lnv, rso = et
                P.op("dve", lambda e: e.reciprocal(out=r1[:, :], in_=B[5][:, :]), reads=[Br[5]], writes=[etr[0]])
                P.op("dve", lambda e: e.reciprocal(out=r2[:, :], in_=B[6][:, :]), reads=[Br[6]], writes=[etr[1]])
                P.op("dve", lambda e: e.tensor_tensor(out=a_[:, :], in0=B[3][:, :], in1=r1[:, :], op=ALU.mult), reads=[Br[3], etr[0]], writes=[etr[2]])
                P.op("dve", lambda e: e.tensor_tensor(out=b_[:, :], in0=B[4][:, :], in1=r2[:, :], op=ALU.mult), reads=[Br[4], etr[1]], writes=[etr[3]])
                P.op("dve", lambda e: e.scalar_tensor_tensor(out=o_[:, :], in0=b_[:, :], scalar=nlam, in1=a_[:, :], op0=ALU.mult, op1=ALU.add),
                     reads=[etr[2], etr[3], lr], writes=[etr[4]])
                P.op("dve", lambda e: e.tensor_tensor(out=osq[:, :], in0=o_[:, :], in1=o_[:, :], op=ALU.mult), reads=[etr[4]], writes=[etr[5]])
                P.op("pe", lambda e: e.matmul(B[7][:, :], lhsT=ones_f[:, :], rhs=osq[:, :], start=True, stop=True), reads=[etr[5], Cr], writes=[Br[7]])
                P.op("act", lambda e: e.activation(out=lnv[:, :], in_=B[7][:, :], func=AF.Ln, bias=vecs[:, VC_EPS:VC_EPS + 1], scale=1.0 / 128),
                     reads=[Br[7], Cr], writes=[etr[6]])
                P.op("act", lambda e: e.activation(out=rso[:, :], in_=lnv[:, :], func=AF.Exp, scale=-0.5), reads=[etr[6]], writes=[etr[7]])
                P.op("dve", lambda e, qt=qt: e.scalar_tensor_tensor(out=OT[:, qt * 512:(qt + 1) * 512], in0=o_[:, :], scalar=dhg08, in1=rso[:, :],
                                                                   op0=ALU.mult, op1=ALU.mult),
                     reads=[etr[4], etr[7], lr], writes=[OTr[qt]])

            tap(32768, OT[:, :], 8192, OTr)
            s_st2 = P.newsem("d_st2")
            st2 = []
            for j in range(8):
                st2.append(P.dma("sp", [lambda e, j=j: e.dma_start(out=ag2_src[j * 128:(j + 1) * 128, :], in_=OT[:, j * T:(j + 1) * T])],
                                 s_st2, reads=[OTr[2 * j], OTr[2 * j + 1]]))
            s_cc2 = P.newsem("cc_2")
            ag2r = Res()
            P.dma("pool", [lambda e: e.collective_compute("AllGather", ALU.bypass, replica_groups=[list(range(NCORES))],
                                                          ins=[ag2_src[:, :]], outs=[ag2_dst[:, :]])],
                  s_cc2, writes=[ag2r], deps=st2, inc=1)
            barrier()
            ODc = R2("ODc", [128, 8, T], BF16, 0)
            ODr = [[Res(), Res()] for _ in range(8)]
            s_g2 = P.newsem("d_g2")
            P.dma("pool", [(lambda e, r=r: e.indirect_dma_start(out=ODc[:, r, :], out_offset=None, in_=ag2_dst[:, :],
                                                                in_offset=bass.IndirectOffsetOnAxis(ap=idxB[:, r:r + 1], axis=0)))
                           for r in range(8)], s_g2, reads=[ag2r, ixr], writes=[ODr[r][h] for r in range(8) for h in range(2)])
            tap(40960, ODc[:, :, :].rearrange("p a b -> p (a b)"), 8192, [ODr[r][h] for r in range(8) for h in range(2)])
            wout_half(0, ODc, ODr)

        def xattn():
            ones_b = sb("ones_b2", [128, 128], BF16, OFF_C + 7424)
            rmsnorm(VC_GX, 40960)
            MT = R2("MT", [128, 16, 256], F32, 0)
            MH = R2("MH", [128, 16, 256], BF16, 16384)
            sqm = [R2("sqm%d" % a, [128, 256], F32, 24576 + 1024 * a) for a in range(2)]
            sdm = R2("sdm", [128, 256], F32, 26624)
            rsm = R2("rsm", [128, 256], F32, 27648)
            KxT = R2("KxT", [128, 4, 256], BF16, 28672)
            VxT = R2("VxT", [128, 4, 256], BF16, 30720)
            Vx = R2("Vx", [128, 2, 512], BF16, 32768)
            xs = [R2("xs%d" % a, [128, 512], F32, 34816 + 2048 * a) for a in range(3)]
            PTx = [R2("PTx%d" % a, [128, 512], BF16, 40960 + 1024 * a) for a in range(4)]
            QxT = R3("QxT", [128, 4, T], BF16, 0)
            OxT = R3("OxT", [128, 4, T], BF16, 8192)
            mtr, mhr = Res(), [Res() for _ in range(16)]
            sqmr = [Res(), Res()]
            sdmr, rsmr = Res(), Res()
            kxr = [Res() for _ in range(4)]
            vxtr = [Res() for _ in range(4)]
            vxr = Res()
            xsr = [Res() for _ in range(3)]
            qxr = [[Res(), Res()] for _ in range(4)]
            oxr = [[Res(), Res()] for _ in range(4)]
            s_m = P.newsem("d_mem")
            P.dma("sp", [lambda e: e.dma_start(out=MT[:, :, :], in_=memT_d.rearrange("(c p) t -> p c t", p=128))], s_m, writes=[mtr])
            for c in range(16):
                P.op("act", lambda e, c=c: e.activation(out=sqm[c % 2][:, :], in_=MT[:, c, :], func=AF.Square), reads=[mtr], writes=[sqmr[c % 2]])
                P.op("pe", lambda e, c=c: e.matmul(B[7][:, 0:256], lhsT=ones_f[:, :], rhs=sqm[c % 2][:, :], start=(c == 0), stop=(c == 15)),
                     reads=[sqmr[c % 2], Cr], writes=[Br[7]])
            P.op("act", lambda e: e.activation(out=sdm[:, :], in_=B[7][:, 0:256], func=AF.Sqrt, bias=vecs[:, VC_EPS:VC_EPS + 1], scale=1.0 / D),
                 reads=[Br[7], Cr], writes=[sdmr])
            P.op("dve", lambda e: e.reciprocal(out=rsm[:, :], in_=sdm[:, :]), reads=[sdmr], writes=[rsmr])
            for c in range(16):
                P.op("dve", lambda e, c=c: e.scalar_tensor_tensor(out=MH[:, c, :], in0=MT[:, c, :], scalar=vecs[:, VC_GMEM + c:VC_GMEM + c + 1],
                                                                 in1=rsm[:, :], op0=ALU.mult, op1=ALU.mult),
                     reads=[mtr, rsmr, Cr], writes=[mhr[c]])

            def headnorm(bk, n, gcol, out_ap, out_res):
                P.op("act", lambda e: e.activation(out=xs[0][:, 0:n], in_=B[bk][:, 0:n], func=AF.Square), reads=[Br[bk]], writes=[xsr[0]])
                P.op("pe", lambda e: e.matmul(B[6][:, 0:n], lhsT=ones_f[:, :], rhs=xs[0][:, 0:n], start=True, stop=True),
                     reads=[xsr[0], Cr], writes=[Br[6]])
                P.op("act", lambda e: e.activation(out=xs[1][:, 0:n], in_=B[6][:, 0:n], func=AF.Sqrt, bias=vecs[:, VC_EPS:VC_EPS + 1], scale=1.0 / 128),
                     reads=[Br[6], Cr], writes=[xsr[1]])
                P.op("dve", lambda e: e.reciprocal(out=xs[2][:, 0:n], in_=xs[1][:, 0:n]), reads=[xsr[1]], writes=[xsr[2]])
                P.op("dve", lambda e: e.scalar_tensor_tensor(out=out_ap, in0=B[bk][:, 0:n], scalar=vecs[:, gcol:gcol + 1], in1=xs[2][:, 0:n],
                                                             op0=ALU.mult, op1=ALU.mult),
                     reads=[Br[bk], xsr[2], Cr], writes=[out_res])

            for j in range(8):
                w, wr = ws.next(("cols", "xattn_w_kv", j))
                bk = j % 2

                def mm(e, w=w, bk=bk):
                    ins = None
                    for c in range(16):
                        ins = e.matmul(B[bk][:, 0:256], lhsT=w[:, c, :], rhs=MH[:, c, :], start=(c == 0), stop=(c == 15))
                    return ins
                P.op("pe", mm, reads=[wr] + mhr, writes=[Br[bk]])
                ws.release()
                if j < 4:
                    headnorm(bk, 256, VC_XKG, KxT[:, j, :], kxr[j])
                else:
                    h = j - 4
                    P.op("act", lambda e, h=h, bk=bk: e.activation(out=VxT[:, h, :], in_=B[bk][:, 0:256], func=AF.Copy), reads=[Br[bk]], writes=[vxtr[h]])
            for mb in range(2):
                def tr(e, mb=mb):
                    ins = None
                    for h in range(4):
                        ins = e.matmul(B[2 + mb][:, h * 128:(h + 1) * 128], lhsT=VxT[:, h, mb * 128:(mb + 1) * 128], rhs=ident[:, :], start=True, stop=True)
                    return ins
                P.op("pe", tr, reads=vxtr + [Cr2], writes=[Br[2 + mb]])
                P.op("act", lambda e, mb=mb: e.activation(out=Vx[:, mb, :], in_=B[2 + mb][:, :], func=AF.Copy), reads=[Br[2 + mb]], writes=[vxr])
            for h in range(4):
                w, wr = ws.next(("cols", "xattn_w_q", h))
                for half in range(2):
                    proj_group(half, w, wr, half)
                    headnorm(half, 512, VC_XQG, QxT[:, h, sl(half)], qxr[h][half])
                ws.release()
            ptr = [Res() for _ in range(4)]
            pc = 0
            for h in range(4):
                for half in range(2):
                    hs = sl(half)
                    pis = []
                    for mb in range(2):
                        bk = 2 + (pc % 2)
                        pi = pc % 4
                        pc += 1
                        P.op("pe", lambda e, bk=bk, h=h, mb=mb, hs=hs: e.matmul(B[bk][:, :], lhsT=KxT[:, h, mb * 128:(mb + 1) * 128], rhs=QxT[:, h, hs],
                                                                             start=True, stop=True),
                             reads=[kxr[h], qxr[h][half]], writes=[Br[bk]])
                        P.op("act", lambda e, bk=bk, pi=pi: e.activation(out=PTx[pi][:, :], in_=B[bk][:, :], func=AF.Exp, scale=float(128 ** -0.5)),
                             reads=[Br[bk]], writes=[ptr[pi]])
                        pis.append(pi)
                    ob, lb = 4, 5

                    def mo(e, h=h, pis=tuple(pis)):
                        ins = None
                        for mb in range(2):
                            ins = e.matmul(B[4][:, :], lhsT=Vx[:, mb, h * 128:(h + 1) * 128], rhs=PTx[pis[mb]][:, :], start=(mb == 0), stop=(mb == 1))
                        return ins
                    P.op("pe", mo, reads=[vxr] + [ptr[p_] for p_ in pis], writes=[Br[4]])

                    def ml(e, pis=tuple(pis)):
                        ins = None
                        for mb in range(2):
                            ins = e.matmul(B[5][:, :], lhsT=ones_b[:, :], rhs=PTx[pis[mb]][:, :], start=(mb == 0), stop=(mb == 1))
                        return ins
                    P.op("pe", ml, reads=[ptr[p_] for p_ in pis], writes=[Br[5]])
                    P.op("dve", lambda e: e.reciprocal(out=xs[0][:, :], in_=B[5][:, :]), reads=[Br[5]], writes=[xsr[0]])
                    P.op("dve", lambda e, h=h, hs=hs: e.tensor_tensor(out=OxT[:, h, hs], in0=B[4][:, :], in1=xs[0][:, :], op=ALU.mult),
                         reads=[Br[4], xsr[0]], writes=[oxr[h][half]])
            cnt = 0
            for qq in range(4):
                w, wr = ws.next(("wo", qq))
                for ii in range(4):
                    for half in range(2):
                        hs = sl(half)
                        bk = cnt % 2
                        cnt += 1

                        def mm(e, w=w, ii=ii, hs=hs, bk=bk):
                            ins = None
                            for kk in range(4):
                                ins = e.matmul(B[bk][:, :], lhsT=w[:, kk * 4 + ii, :], rhs=OxT[:, kk, hs], start=(kk == 0), stop=(kk == 3))
                            return ins
                        P.op("pe", mm, reads=[wr] + [oxr[kk][half] for kk in range(4)], writes=[Br[bk]])
                        i = 4 * qq + ii
                        P.op("dve", lambda e, i=i, hs=hs, bk=bk: e.tensor_tensor(out=X[:, i, hs], in0=B[bk][:, :], in1=X[:, i, hs], op=ALU.add),
                             reads=[Br[bk]], writes=[Xr[i][half]])
                ws.release()

        def barrier():
            toks = [(k, v) for k, v in P.cnt.items() if v > 0 and not k.startswith("cc_")]
            for e in P.ENG:
                P.wait(e, toks)

        if not SKIP_FFN:
            ffn("ffn1", VC_G1)
        if stage >= 2:
            barrier()
            mixing()
        if stage >= 3:
            barrier()
            xattn()
        if stage >= 4 and not SKIP_FFN:
            barrier()
            ffn("ffn2", VC_G2)

        s_o = P.newsem("d_o")
        tok = P.dma("sp", [(lambda e, c=c: e.dma_start(out=outT[c * 128:(c + 1) * 128, :], in_=X[:, c, :])) for c in range(16)],
                    s_o, reads=[Xr[c][h] for c in range(16) for h in range(2)])
        P.wait("sp", [tok] + dbg_toks)
        P.wait("sp", [("c_" + e, P.cnt["c_" + e]) for e in ("pe", "act", "dve") if P.cnt["c_" + e] > 0])

        with nc.Block() as block:
            P.run(block)
    return nc, plan


def _tile(inputs, spec):
    kind = spec[0]
    if kind == "cols":
        W = inputs[spec[1]][0]
        j = spec[2]
        blk = W[:, j * 128:(j + 1) * 128]
        return np.ascontiguousarray(blk.reshape(16, 128, 128).transpose(1, 0, 2)).reshape(128, 2048)
    if kind == "rows":
        W = inputs[spec[1]][0]
        j = spec[2]
        return W[j * 128:(j + 1) * 128, :]
    if kind == "wout":
        W = inputs["w_out"][0]
        hh, pr = spec[1], spec[2]
        hh_rows = (0 if hh == 0 else 1024)
        blk = W[hh_rows:hh_rows + 1024, pr * 256:(pr + 1) * 256]
        return np.ascontiguousarray(blk.reshape(8, 128, 2, 128).transpose(1, 0, 2, 3)).reshape(128, 2048)
    if kind == "wo":
        W = inputs["xattn_w_o"][0]
        qq = spec[1]
        blk = W[:, qq * 512:(qq + 1) * 512]
        return np.ascontiguousarray(blk.reshape(4, 128, 4, 128).transpose(1, 0, 2, 3)).reshape(128, 2048)
    raise ValueError(spec)


def _host_consts():
    cst = np.zeros((128, 384 + 128 + 2048), np.float32)
    cst[:, 0:128] = 1.0
    for b in (0, 64):
        cst[b:b + 64, 128 + b:128 + b + 64] = 1.0
    prot = np.zeros((128, 128), np.float32)
    for b in (0, 64):
        for i in range(8):
            prot[b + 8 + i, b + i] = -1.0
            prot[b + i, b + 8 + i] = 1.0
    cst[:, 256:384] = prot
    cst[:, 384:512] = np.eye(128, dtype=np.float32)
    k = np.arange(128)[:, None]
    q = np.arange(512)[None, :]
    for j in range(4):
        cst[:, 512 + j * 512:512 + (j + 1) * 512] = (q >= 128 * j + k).astype(np.float32)
    return cst


def _host_vecs(inputs, core):
    v = np.zeros((128, NVEC), np.float32)
    def colmajor(g):
        return np.asarray(g).reshape(16, 128).T
    v[:, VC_G1:VC_G1 + 16] = colmajor(inputs["ffn1_norm_g"][0])
    v[:, VC_GM:VC_GM + 16] = colmajor(inputs["mix_norm_g"][0])
    v[:, VC_GX:VC_GX + 16] = colmajor(inputs["xattn_norm_g"][0])
    v[:, VC_GMEM:VC_GMEM + 16] = colmajor(inputs["mem_norm_g"][0])
    v[:, VC_G2:VC_G2 + 16] = colmajor(inputs["ffn2_norm_g"][0])
    v[:, VC_QG] = np.tile(inputs["q_norm_g"][0], 2)
    v[:, VC_KG] = np.tile(inputs["k_norm_g"][0], 2)
    v[:64, VC_LAM + 0] = inputs["lambda_q1"][0]
    v[:64, VC_LAM + 1] = inputs["lambda_k1"][0]
    v[:64, VC_LAM + 2] = inputs["lambda_q2"][0]
    v[:64, VC_LAM + 3] = inputs["lambda_k2"][0]
    v[:, VC_DHG] = inputs["diff_head_norm_g"][0]
    cw = np.asarray(inputs["conv_w"][0])
    v[:, VC_CW:VC_CW + 248] = cw.reshape(31, 8, 128).transpose(2, 1, 0).reshape(128, 248)
    v[:, VC_CB:VC_CB + 8] = np.asarray(inputs["conv_b"][0]).reshape(8, 128).T
    v[:, VC_LNG:VC_LNG + 8] = np.asarray(inputs["conv_ln_g"][0]).reshape(8, 128).T
    v[:, VC_LNB:VC_LNB + 8] = np.asarray(inputs["conv_ln_b"][0]).reshape(8, 128).T
    v[:, VC_XQG] = inputs["xattn_q_norm_g"][0]
    v[:, VC_XKG] = inputs["xattn_k_norm_g"][0]
    invf = np.zeros(128, np.float32)
    fr = (np.float32(500000.0) ** (-np.arange(0, 16, 2, dtype=np.float32) / np.float32(16))).astype(np.float32)
    for b in (0, 64):
        invf[b:b + 8] = fr
        invf[b + 8:b + 16] = fr
    v[:, VC_INVF] = invf
    v[:, VC_HM] = 0.0 if core == 0 else 1.0
    v[:, VC_EPS] = EPS
    return v


_CACHE = {}


def run(inputs, stage=99):
    inputs = {k: np.asarray(v) for k, v in inputs.items()}
    key = ("nc", stage)
    if key not in _CACHE:
        _CACHE[key] = build(stage)
    nc, plan = _CACHE[key]
    wall = np.empty((len(plan), 128, 2048), np.float32)
    for i, spec in enumerate(plan):
        wall[i] = _tile(inputs, spec)
    cst = _host_consts()
    x = inputs["x"][0]
    memT = np.ascontiguousarray(inputs["mem"][0].T)
    pos = inputs["positions"][0].astype(np.int32)
    in_maps = []
    for c in range(NCORES):
        p = np.arange(128, dtype=np.int32)[:, None]
        idxA = np.zeros((128, 24), np.int32)
        for r in range(8):
            for wh in range(3):
                idxA[:, r * 3 + wh] = (((r * 3 + wh) * 8 + c) * 128 + p[:, 0])
        idxB = np.zeros((128, 8), np.int32)
        for r in range(8):
            idxB[:, r] = ((r * 8 + c) * 128 + p[:, 0])
        idxH = np.zeros((128, 8), np.int32)
        pc = (c - 1) % 8
        for i in range(8):
            idxH[:, i] = ((pc * 8 + i) * 128 + p[:, 0])
        in_maps.append({
            "xT": np.ascontiguousarray(x[c * T:(c + 1) * T, :].T),
            "wall": wall,
            "vecs": _host_vecs(inputs, c),
            "cst": cst,
            "posr": np.ascontiguousarray(np.broadcast_to(pos[c * T:(c + 1) * T][None, :], (128, T))),
            "memT": memT,
            "idxA": idxA, "idxB": idxB, "idxH": idxH,
        })
    res = run_bass_kernel_spmd(nc, in_maps, core_ids=list(range(NCORES)))
    out = np.empty((1, S, D), np.float32)
    for c in range(NCORES):
        out[0, c * T:(c + 1) * T, :] = np.asarray(res.results[c]["outT"]).T
    if DEBUG:
        _CACHE["dbg"] = [np.asarray(res.results[c]["dbg"]) for c in range(NCORES)]
    return out


def kernel(**inputs):
    return run(inputs)
```
